# Optimizing a Trainium2 kernel written in Bass

```python
import jax, jax.numpy as jnp
from jax import lax
import numpy as np

D_MODEL = 1024
BATCH = 16
SEQ = 2048
DEPTH = 1

GRID_W = 64
CTX_LEN = 256
FOURIER_WIDTH = D_MODEL // 4
FOURIER_GROUPS = 4
FOURIER_GROUP_DIM = FOURIER_WIDTH // FOURIER_GROUPS
RET_WIDTH = D_MODEL - FOURIER_WIDTH
RET_HEADS = 6
RET_V_DIM = RET_WIDTH // RET_HEADS
RET_QK_DIM = RET_V_DIM // 2
RET_QK_WIDTH = RET_HEADS * RET_QK_DIM
RET_CHUNK = 128
ROPE_BASE = 10000.0
D_FF = -(-8 * D_MODEL // (3 * 256)) * 256
PROJ_WIDTH = FOURIER_WIDTH + 2 * RET_QK_WIDTH + 2 * RET_WIDTH
LN_EPS = 1e-6
DEEPNORM_ALPHA = (2.0 * DEPTH) ** 0.25
DEEPNORM_BETA = (8.0 * DEPTH) ** -0.25

kernel_name = "hymba_fnet_retnet_deepnorm_dit"

F32 = jnp.float32


def layer_norm(x, g=None, b=None):
    x32 = x.astype(F32)
    mu = jnp.mean(x32, axis=-1, keepdims=True)
    var = jnp.mean(jnp.square(x32 - mu), axis=-1, keepdims=True)
    y = (x32 - mu) * lax.rsqrt(var + LN_EPS)
    if g is not None:
        y = y * g.astype(F32) + b.astype(F32)
    return y.astype(x.dtype)


def modulate(xn, shift, scale):
    return xn * (1.0 + scale) + shift


def rope_1d(x, pos):
    nf = x.shape[-1] // 2
    freqs = ROPE_BASE ** (-jnp.arange(nf, dtype=F32) / nf)
    ang = pos[:, None] * freqs[None, :]
    cos = jnp.cos(ang)[None, :, None, :]
    sin = jnp.sin(ang)[None, :, None, :]
    x32 = x.astype(F32)
    x1, x2 = x32[..., :nf], x32[..., nf:]
    return jnp.concatenate([x1 * cos - x2 * sin, x1 * sin + x2 * cos], axis=-1).astype(x.dtype)


def axial_rope(x, row_pos, col_pos):
    half = x.shape[-1] // 2
    return jnp.concatenate([rope_1d(x[..., :half], row_pos), rope_1d(x[..., half:], col_pos)], axis=-1)


def chunk_retention(q, k, v, log_gamma, state0, strict):
    B, N, H, dk = q.shape
    dv = v.shape[-1]
    nC = N // RET_CHUNK
    idx = jnp.arange(RET_CHUNK, dtype=F32)
    diff = idx[:, None] - idx[None, :]
    mask = (diff > 0) if strict else (diff >= 0)
    safe = jnp.where(mask, diff, 0.0)
    decay_inner = jnp.where(mask[None], jnp.exp(log_gamma[:, None, None] * safe[None]), 0.0)
    xi = jnp.exp(log_gamma[:, None] * (idx[None, :] + 1.0))
    zeta = jnp.exp(log_gamma[:, None] * (RET_CHUNK - 1.0 - idx[None, :]))
    g_chunk = jnp.exp(log_gamma * RET_CHUNK)

    def to_chunks(t, d):
        return t.astype(F32).reshape(B, nC, RET_CHUNK, H, d).transpose(1, 0, 3, 2, 4)

    qc, kc, vc = to_chunks(q, dk), to_chunks(k, dk), to_chunks(v, dv)

    def step(state, inp):
        qi, ki, vi = inp
        scores = jnp.einsum('bhcd,bhld->bhcl', qi, ki) * decay_inner[None]
        inner = jnp.einsum('bhcl,bhle->bhce', scores, vi)
        cross = jnp.einsum('bhcd,bhde->bhce', qi, state) * xi[None, :, :, None]
        new_state = g_chunk[None, :, None, None] * state + jnp.einsum(
            'bhld,bhle->bhde', ki * zeta[None, :, :, None], vi)
        return new_state, inner + cross

    final, out = lax.scan(step, state0, (qc, kc, vc))
    out = out.transpose(1, 0, 3, 2, 4).reshape(B, N, H, dv)
    return out, final


def bidirectional_retention(q, k, v, lg_f, lg_b, state_f, state_b):
    rev = lambda t: jnp.flip(t, axis=1)
    y_f, fin_f = chunk_retention(q, k, v, lg_f, state_f, False)
    y_b, fin_b = chunk_retention(rev(q), rev(k), rev(v), lg_b, state_b, True)
    return y_f + rev(y_b), fin_f, fin_b


def mixer_inputs(u, w_in):
    p = u @ w_in
    f, q, k, v, g = jnp.split(p, [FOURIER_WIDTH, FOURIER_WIDTH + RET_QK_WIDTH,
                                  FOURIER_WIDTH + 2 * RET_QK_WIDTH,
                                  FOURIER_WIDTH + 2 * RET_QK_WIDTH + RET_WIDTH], axis=-1)
    B, N = u.shape[0], u.shape[1]
    q = q.reshape(B, N, RET_HEADS, RET_QK_DIM)
    k = k.reshape(B, N, RET_HEADS, RET_QK_DIM) * (RET_QK_DIM ** -0.5)
    v = v.reshape(B, N, RET_HEADS, RET_V_DIM)
    return f, q, k, v, g


def fourier_mix(f):
    B, N, _ = f.shape
    fg = f.reshape(B, N, FOURIER_GROUPS, FOURIER_GROUP_DIM).astype(F32)
    out = jnp.fft.fft2(fg, axes=(1, 3), norm="ortho").real
    return out.reshape(B, N, FOURIER_WIDTH).astype(f.dtype)


def gated_group_norm(y, g):
    B, N = y.shape[0], y.shape[1]
    y = y * lax.rsqrt(jnp.mean(jnp.square(y), axis=-1, keepdims=True) + LN_EPS)
    return (y.reshape(B, N, RET_WIDTH) * jax.nn.silu(g.astype(F32))).astype(g.dtype)


def swiglu(u, wg, wu, wd):
    return (jax.nn.silu(u @ wg) * (u @ wu)) @ wd


def setup_inputs(seed: int = 0) -> dict:
    key = jax.random.key(seed)
    ks = jax.random.split(key, 16)
    nrm = lambda k, shape, s: jax.random.normal(k, shape, F32) * s
    h = jnp.arange(RET_HEADS, dtype=F32)
    p = 2.0 ** (-5.0 - h)
    decay_logit = jnp.log((1.0 - p) / p)
    return {
        "x": nrm(ks[0], (BATCH, SEQ, D_MODEL), 1.0),
        "c": nrm(ks[1], (BATCH, D_MODEL), 1.0),
        "ctx": nrm(ks[2], (BATCH, CTX_LEN, D_MODEL), 1.0),
        "c_ctx": nrm(ks[3], (D_MODEL,), 1.0),
        "w_mod": nrm(ks[4], (DEPTH, D_MODEL, 6 * D_MODEL), 0.5 * D_MODEL ** -0.5),
        "b_mod": nrm(ks[5], (DEPTH, 6 * D_MODEL), 0.01),
        "w_in": nrm(ks[6], (DEPTH, D_MODEL, PROJ_WIDTH), D_MODEL ** -0.5),
        "w_out": nrm(ks[7], (DEPTH, D_MODEL, D_MODEL), DEEPNORM_BETA * D_MODEL ** -0.5),
        "decay_fwd": decay_logit[None, :] + nrm(ks[8], (DEPTH, RET_HEADS), 0.1),
        "decay_bwd": decay_logit[None, :] + nrm(ks[9], (DEPTH, RET_HEADS), 0.1),
        "ln1_g": 1.0 + nrm(ks[10], (DEPTH, D_MODEL), 0.02),
        "ln1_b": nrm(ks[11], (DEPTH, D_MODEL), 0.02),
        "w_ffn_gate": nrm(ks[12], (DEPTH, D_MODEL, D_FF), D_MODEL ** -0.5),
        "w_ffn_up": nrm(ks[13], (DEPTH, D_MODEL, D_FF), D_MODEL ** -0.5),
        "w_ffn_down": nrm(ks[14], (DEPTH, D_FF, D_MODEL), DEEPNORM_BETA * D_FF ** -0.5),
        "ln2_g": 1.0 + nrm(ks[15], (DEPTH, D_MODEL), 0.02),
        "ln2_b": nrm(jax.random.fold_in(ks[15], 1), (DEPTH, D_MODEL), 0.02),
    }


def reference(x, c, ctx, c_ctx, w_mod, b_mod, w_in, w_out, decay_fwd, decay_bwd,
              ln1_g, ln1_b, w_ffn_gate, w_ffn_up, w_ffn_down, ln2_g, ln2_b):
    B, N, _ = x.shape
    rows = N // GRID_W
    row_pos = jnp.repeat(jnp.arange(rows), GRID_W).astype(F32)
    col_pos = jnp.tile(jnp.arange(GRID_W), rows).astype(F32)
    state0 = jnp.zeros((B, RET_HEADS, RET_QK_DIM, RET_V_DIM), F32)

    for l in range(DEPTH):
        last = l == DEPTH - 1
        mod_x = jax.nn.silu(c) @ w_mod[l] + b_mod[l]
        mod_c = jax.nn.silu(c_ctx) @ w_mod[l] + b_mod[l]
        sh1, sc1, gt1, sh2, sc2, gt2 = jnp.split(mod_x[:, None, :], 6, axis=-1)
        csh1, csc1, cgt1, csh2, csc2, cgt2 = jnp.split(mod_c[None, None, :], 6, axis=-1)
        lg_f = jax.nn.log_sigmoid(decay_fwd[l].astype(F32))
        lg_b = jax.nn.log_sigmoid(decay_bwd[l].astype(F32))

        uc = modulate(layer_norm(ctx), csh1, csc1)
        fc, qc, kc, vc, gc = mixer_inputs(uc, w_in[l])
        yc, st_f, st_b = bidirectional_retention(qc, kc, vc, lg_f, lg_b, state0, state0)

        u = modulate(layer_norm(x), sh1, sc1)
        f, q, k, v, g = mixer_inputs(u, w_in[l])
        q = axial_rope(q, row_pos, col_pos)
        k = axial_rope(k, row_pos, col_pos)
        y, _, _ = bidirectional_retention(q, k, v, lg_f, lg_b, st_f, st_b)
        mix = jnp.concatenate([fourier_mix(f), gated_group_norm(y, g)], axis=-1) @ w_out[l]
        x = layer_norm(DEEPNORM_ALPHA * x + gt1 * mix, ln1_g[l], ln1_b[l])
        u2 = modulate(layer_norm(x), sh2, sc2)
        x = layer_norm(DEEPNORM_ALPHA * x + gt2 * swiglu(u2, w_ffn_gate[l], w_ffn_up[l], w_ffn_down[l]),
                       ln2_g[l], ln2_b[l])

        if not last:
            mix_c = jnp.concatenate([fourier_mix(fc), gated_group_norm(yc, gc)], axis=-1) @ w_out[l]
            ctx = layer_norm(DEEPNORM_ALPHA * ctx + cgt1 * mix_c, ln1_g[l], ln1_b[l])
            uc2 = modulate(layer_norm(ctx), csh2, csc2)
            ctx = layer_norm(DEEPNORM_ALPHA * ctx + cgt2 * swiglu(uc2, w_ffn_gate[l], w_ffn_up[l], w_ffn_down[l]),
                             ln2_g[l], ln2_b[l])

    return x
```

```python
import os
import numpy as np
import ml_dtypes
import concourse.bass as bass
import concourse.mybir as mybir
from concourse.bass_utils import run_bass_kernel_spmd

F32 = mybir.dt.float32
BF16 = mybir.dt.bfloat16
ALU = mybir.AluOpType
AF = mybir.ActivationFunctionType
AX = mybir.AxisListType

D = 1024
N = 2048
NT = 16
CTX = 256
H = 6
DFF = 2816
NJ = 22
ALPHA = float(2.0 ** 0.25)
EPS = 1e-6
KBY = 1024
LN8 = float(np.log(0.125))


class Res:
    __slots__ = ("name", "lw", "rd", "dsem", "dval", "fill", "persist")

    def __init__(self, name):
        self.name = name
        self.persist = False
        self.lw = None
        self.rd = []
        self.fill = []
        self.dsem = None
        self.dval = 0


class Op:
    __slots__ = ("eng", "kind", "fns", "out", "in_", "kw", "reads", "writes", "dur", "idx", "deps", "ndep", "users",
                 "start", "finish", "tok", "semres", "gen")


class Eng:
    def __init__(self, K, name):
        self.K = K
        self.name = name
        self.ops = []
        self.sem = K.new_sem("p_" + name)
        self.n = 0
        self.waited = {}
        self.ninstr = 0
        self.t = 0.0


def _cost(f):
    return getattr(f, "cost", 0.3)


class KB:
    def __init__(self, nc):
        self.nc = nc
        self._stack = []
        self._semstack = []
        self._res = {}
        self.sems = []
        self.dres = []
        self.pending = []
        self.gen = 0
        self.pe = Eng(self, "pe")
        self.act = Eng(self, "act")
        self.dve = Eng(self, "dve")
        self.pool = Eng(self, "pool")
        self.sp = Eng(self, "sp")
        self.engs = [self.pe, self.act, self.dve, self.pool, self.sp]

    def new_sem(self, name):
        cm = self.nc.semaphore(name + "_%d" % len(self.sems))
        s = cm.__enter__()
        self._semstack.append(cm)
        self.sems.append(s)
        return s

    def res(self, name):
        r = self._res.get(name)
        if r is None:
            r = Res(name)
            self._res[name] = r
        return r

    def psum(self, name, shape, dtype):
        self._uid = getattr(self, "_uid", 0) + 1
        cm = self.nc.psum_tensor("%s_u%d" % (name, self._uid), list(shape), dtype)
        t = cm.__enter__()
        self._stack.append(cm)
        return t

    def mark(self):
        return len(self._stack)

    def release(self, mark):
        while len(self._stack) > mark:
            cm = self._stack.pop()
            cm.__exit__(None, None, None)

    def _record(self, o, reads, writes, after=()):
        o.reads = list(reads)
        o.writes = list(writes)
        o.idx = len(self.pending)
        o.gen = self.gen
        deps = set()

        def addw(res):
            if res.lw is not None and (res.lw.gen == self.gen or res.persist):
                deps.add(res.lw)
                if res.lw.kind == "d":
                    for x in res.fill:
                        if x.gen == self.gen or res.persist:
                            deps.add(x)

        for x in after:
            deps.add(x)

        for r in o.reads:
            addw(r)
        samefill = {}
        for w in o.writes:
            sf = (o.kind == "d" and w.lw is not None and w.lw.kind == "d" and not w.rd and w.lw.gen == self.gen)
            samefill[id(w)] = sf
            if not sf:
                addw(w)
            elif w.fill:
                for x in w.fill[0].deps:
                    if x.gen == self.gen:
                        deps.add(x)
            for x in w.rd:
                if x.gen == self.gen:
                    deps.add(x)
        deps.discard(o)
        o.deps = list(deps)
        o.users = []
        for r in o.reads:
            r.rd.append(o)
        for w in o.writes:
            if o.kind == "d":
                if samefill[id(w)]:
                    w.fill.append(o)
                else:
                    w.fill = [o]
            else:
                w.fill = []
            w.lw = o
            w.rd = []
        self.pending.append(o)

    def op(self, eng, fns, reads=(), writes=(), after=()):
        if callable(fns):
            fns = [fns]
        o = Op()
        o.eng = eng
        o.kind = "c"
        o.fns = fns
        o.dur = sum(_cost(f) for f in fns) * (2.6 if eng is self.pool else 1.0)
        self._record(o, reads, writes, after)
        return o

    def dma(self, eng, out, in_, reads=(), writes=(), semres=None, after=(), **kw):
        if semres is None:
            semres = writes[0] if writes else reads[0]
        if semres.dsem is None:
            semres.dsem = self.new_sem("d_" + semres.name)
            self.dres.append(semres)
        o = Op()
        o.eng = eng
        o.kind = "d"
        o.out, o.in_, o.kw = out, in_, kw
        o.semres = semres
        try:
            nbytes = out.nbytes()
        except Exception:
            nbytes = 65536
        o.dur = 2.0 + nbytes / 160e3
        self._record(o, reads, writes, after)
        return o

    def flush(self):
        ops = self.pending
        if not ops:
            return
        for o in ops:
            o.ndep = sum(1 for d in o.deps if d.gen == self.gen)
            o.start = None
        for o in ops:
            for d in o.deps:
                if d.gen == self.gen:
                    d.users.append(o)
                else:
                    d.finish = 0.0
        ready = {e: [] for e in self.engs}
        for o in ops:
            if o.ndep == 0:
                ready[o.eng].append(o)
        for e in self.engs:
            e.t = 0.0
        dma_pipe = 0.0
        nleft = len(ops)
        order = {e: [] for e in self.engs}
        SYNC = 0.12
        sg = os.environ.get("SCHED_GENS")
        SCHED = int(os.environ.get("SCHED", "1")) and ((str(self.gen) in sg.split(",")) if sg else True)
        while nleft:
            best = None
            for e in self.engs:
                rl = ready[e]
                if not rl:
                    continue
                et = e.t
                for o in rl:
                    rt = et
                    for d in o.deps:
                        f = d.finish + SYNC
                        if f > rt:
                            rt = f
                    key = (rt, o.idx) if SCHED else (o.idx, o.idx)
                    if best is None or key < best[0]:
                        best = (key, o, e, rt)
            _, o, e, st = best
            ready[e].remove(o)
            o.start = st
            if o.kind == "d":
                e.t = st + 0.08
                tstart = max(st + 0.5, dma_pipe)
                dma_pipe = tstart + (o.dur - 2.0)
                o.finish = dma_pipe + 1.5
            else:
                o.finish = st + o.dur
                e.t = o.finish
            order[e].append(o)
            nleft -= 1
            for u in o.users:
                u.ndep -= 1
                if u.ndep == 0:
                    ready[u.eng].append(u)
        allo = sorted(ops, key=lambda o: (o.start, o.idx))
        for o in allo:
            if o.kind == "d":
                o.semres.dval += 16
                o.tok = (o.semres.dsem, o.semres.dval)
            else:
                o.eng.n += 1
                o.tok = (o.eng.sem, o.eng.n)
        for e in self.engs:
            for o in order[e]:
                need = {}
                for d in o.deps:
                    s, v = d.tok
                    if e is self.pe and s is e.sem:
                        continue
                    if need.get(s, 0) < v:
                        need[s] = v
                waits = []
                for s, v in need.items():
                    if e.waited.get(s, 0) < v:
                        e.waited[s] = v
                        waits.append((s, v))
                if o.kind == "d":
                    def emit(en, waits=waits, dsem=o.tok[0], out=o.out, in_=o.in_, kw=o.kw):
                        for s, v in waits:
                            en.wait_ge(s, v)
                        en.dma_start(out=out, in_=in_, **kw).then_inc(dsem, 16)
                    e.ninstr += 1 + len(waits)
                else:
                    def emit(en, waits=waits, fns=o.fns, sem=e.sem):
                        for s, v in waits:
                            en.wait_ge(s, v)
                        for f in fns[:-1]:
                            f(en)
                        fns[-1](en).then_inc(sem, 1)
                    e.ninstr += len(o.fns) + len(waits)
                e.ops.append(emit)
        if os.environ.get("SCHED_DUMP") == str(self.gen):
            for e in self.engs:
                print("== eng", e.name)
                for o in order[e][:60]:
                    print("   idx %4d %s start %8.2f fin %8.2f deps %s tok %s" % (o.idx, o.kind, o.start, o.finish, sorted(d.idx for d in o.deps), o.tok[1]))
        self.sim_time = getattr(self, "sim_time", 0.0) + max(o.finish for o in ops)
        self.pending = []
        self.gen += 1

    def barrier(self):
        self.flush()
        sp = self.sp
        waits = []
        for r in self.dres:
            if r.persist:
                continue
            if r.dval > 0 and sp.waited.get(r.dsem, 0) < r.dval:
                sp.waited[r.dsem] = r.dval
                waits.append((r.dsem, r.dval))
        sp.n += 1
        sem = sp.sem

        def emit_sp(e, waits=waits, sem=sem):
            for s, v in waits:
                e.wait_ge(s, v)
            e.nop().then_inc(sem, 1)

        sp.ops.append(emit_sp)
        toks = [(e.sem, e.n) for e in self.engs if e.n > 0]
        for e in self.engs:
            ws = []
            for s, v in toks:
                if e.waited.get(s, 0) < v:
                    e.waited[s] = v
                    ws.append((s, v))

            def emit(en, ws=ws):
                for s, v in ws:
                    en.wait_ge(s, v)

            e.ops.append(emit)

    def emit(self):
        self.flush()
        nc = self.nc
        with nc.Block() as block:
            @block.tensor
            def _(e):
                for f in self.pe.ops:
                    f(e)

            @block.scalar
            def _(e):
                for f in self.act.ops:
                    f(e)

            @block.vector
            def _(e):
                for f in self.dve.ops:
                    f(e)

            @block.gpsimd
            def _(e):
                for f in self.pool.ops:
                    f(e)

            @block.sync
            def _(e):
                for f in self.sp.ops:
                    f(e)

    def close(self):
        self.release(0)
        while self._semstack:
            self._semstack.pop().__exit__(None, None, None)


def _fs(ap):
    try:
        return float(ap.free_size())
    except Exception:
        return 512.0


def _c(f, cost):
    f.cost = cost
    return f


def MM(out, lhsT, rhs, start=True, stop=True):
    return _c(lambda e: e.matmul(out, lhsT=lhsT, rhs=rhs, start=start, stop=stop), max(_fs(rhs), 64.0) / 2300.0 + 0.004)


def TRP(out, in_, ident):
    return _c(lambda e: e.transpose(out, in_, ident), 0.11)


def ACTF(out, in_, func, **kw):
    return _c(lambda e: e.activation(out=out, in_=in_, func=func, **kw), 0.2 + _fs(out) / 1300.0)


def TT(out, in0, in1, op):
    return _c(lambda e: e.tensor_tensor(out=out, in0=in0, in1=in1, op=op), 0.08 + _fs(out) / 900.0)


def TS(out, in0, s1, s2, op0, op1):
    return _c(lambda e: e.tensor_scalar(out=out, in0=in0, scalar1=s1, scalar2=s2, op0=op0, op1=op1), 0.08 + _fs(out) / 900.0)


def TSM(out, in0, s1):
    return _c(lambda e: e.tensor_scalar_mul(out=out, in0=in0, scalar1=s1), 0.08 + _fs(out) / 900.0)


def TSA(out, in0, s1):
    return _c(lambda e: e.tensor_scalar_add(out=out, in0=in0, scalar1=s1), 0.08 + _fs(out) / 900.0)


def STT(out, in0, scalar, in1, op0, op1):
    return _c(lambda e: e.scalar_tensor_tensor(out=out, in0=in0, scalar=scalar, in1=in1, op0=op0, op1=op1), 0.08 + _fs(out) / 900.0)


def CP(out, in_):
    return _c(lambda e: e.tensor_copy(out=out, in_=in_), 0.08 + _fs(out) / 900.0)


def MSET(ap, v):
    return _c(lambda e: e.memset(ap, v), 0.08 + _fs(ap) / 900.0)


def build_program(nb=2, dbg=(), phases="0ABCDE"):
    nc = bass.Bass("TRN2", target_bir_lowering=False)
    K = KB(nc)

    def din(name, shape, dt=F32):
        return nc.dram_tensor(name, list(shape), dt, kind="ExternalInput")

    x_d = din("x", [2, N, D]).ap()
    c_d = din("c3", [3, D]).ap()
    ctx_d = din("ctx", [2, CTX, D]).ap()
    wmod_d = din("w_mod", [D, 6 * D]).ap()
    bmod_d = din("b_mod", [1, 6 * D]).ap()
    win_d = din("w_in", [D, 2560]).ap()
    wout_d = din("w_out", [D, D]).ap()
    dec_d = din("decay", [1, 12]).ap()
    ln_d = din("lnp", [4, D]).ap()
    wg_d = din("w_g", [D, DFF]).ap()
    wu_d = din("w_u", [D, DFF]).ap()
    wd_d = din("w_d", [DFF, D]).ap()
    rc_d = din("rope_cos", [128, 16 * 64]).ap()
    rs_d = din("rope_sin", [128, 16 * 64]).ap()
    dft_d = din("dft", [4, N // 2, N // 2], BF16).ap()
    cbd_d = din("cbd", [2, 256, 256]).ap()
    tab_d = din("tabs", [128, 4 * 128 + 4]).ap()
    id_d = din("ident", [128, 128]).ap()
    out_d = nc.dram_tensor("out", [2, N, D], F32, kind="ExternalOutput").ap()
    wgu_s = nc.dram_tensor("wgu_s", [NJ, 128, 2, 8, 128], BF16, kind="Internal").ap()
    win_s = nc.dram_tensor("win_s", [128, 8, 2560], BF16, kind="Internal").ap()
    dbg_d = {}
    for name, shape, dt_ in dbg:
        dbg_d[name] = nc.dram_tensor("dbg_" + name, list(shape), dt_, kind="ExternalOutput").ap()

    pe, act, dve, pool, sp = K.pe, K.act, K.dve, K.pool, K.sp

    _tc = [0]

    def T(name, shape, dtype, off):
        _tc[0] += 1
        return nc.alloc_sbuf_tensor_at("%s_t%d" % (name, _tc[0]), list(shape), dtype, offset=16512 + int(off))

    ident = T("ident", [128, 128], BF16, 0); R_ident = K.res("ident")
    modfm = T("modfm", [128, 3, 4, 8], F32, 256); R_modfm = K.res("modfm")
    sm = T("sm", [128, 64], F32, 640); R_sm = K.res("sm")
    XIF = T("XIF", [128, 3, 128], F32, 1024)
    XIB = T("XIB", [128, 3, 128], F32, 2560)
    DT = T("DT", [128, 6, 128], F32, 4096); R_tabs = K.res("tabs")
    gtbc = T("gtbc", [128, 2, 2, 1024], F32, 7168); R_gtbc = K.res("gtbc")
    ecol = T("ecol", [128, 4], F32, 23552)
    LGBC = sm[:, 0:12]
    EPSC = sm[:, 60:61]
    NHC = sm[:, 61:62]

    wi = T("wi", [128, 8, 2560], BF16, 134 * KBY); R_wi = K.res("wi")

    def dump(name, src_ap, res, eng=None):
        if name in dbg_d:
            K.dma(sp, dbg_d[name], src_ap, reads=[res])

    R_wgu = K.res("wgu_s")

    R_wins = K.res("win_s")
    WIN_SEGS = [(0, 256, 384), (384, 640, 384), (768, 1024, 768), (1536, 0, 256), (1792, 1792, 768)]

    R_wgu.persist = True
    R_wins.persist = True

    def precast_list():
        L = []
        wvv = win_d.rearrange("(kc p) c -> p kc c", p=128)
        for (d0, s0, w) in WIN_SEGS:
            L.append((win_s[:, :, d0:d0 + w], wvv[:, :, s0:s0 + w], R_wins))
        wgv = wg_d.rearrange("(kc p) (j f) -> j p kc f", p=128, f=128)
        wuv = wu_d.rearrange("(kc p) (j f) -> j p kc f", p=128, f=128)
        for j in range(NJ):
            L.append((wgu_s[j, :, 0, :, :], wgv[j], R_wgu))
            L.append((wgu_s[j, :, 1, :, :], wuv[j], R_wgu))
        return L

    PRE = precast_list()

    def phase0():
        m0 = K.mark()
        wv0 = win_d.rearrange("(kc p) c -> p kc c", p=128)
        for (d0, s0, w) in WIN_SEGS:
            K.dma(pool, wi[:, :, d0:d0 + w], wv0[:, :, s0:s0 + w], writes=[R_wi])
        tabs = T("tabs", [128, 516], F32, 30 * KBY); R_t = K.res("tabsin")
        idf = T("idf", [128, 128], F32, 33 * KBY); R_idf = K.res("idf")
        K.dma(sp, tabs[:], tab_d, writes=[R_t])
        K.dma(sp, idf[:], id_d, writes=[R_idf])
        K.dma(sp, sm[:, 48:60], dec_d.partition_broadcast(128), writes=[R_sm])
        K.op(pool, CP(ident[:], idf[:]), reads=[R_idf], writes=[R_ident])
        K.op(dve, CP(ecol[:], tabs[:, 512:516]), reads=[R_t], writes=[R_tabs])
        K.op(dve, [MSET(sm[:, 60:61], EPS), MSET(sm[:, 61:62], -0.5)], writes=[R_sm])
        K.op(act, ACTF(sm[:, 36:48], sm[:, 48:60], AF.Exp, scale=-1.0), reads=[R_sm], writes=[R_sm])
        K.op(act, ACTF(sm[:, 36:48], sm[:, 36:48], AF.Ln, bias=1.0), reads=[R_sm], writes=[R_sm])
        K.op(dve, TSM(sm[:, 0:12], sm[:, 36:48], -1.0), reads=[R_sm], writes=[R_sm])
        lgv = sm[:, 0:12].rearrange("p (d h two) -> p d h two", d=2, two=2)
        lgp = sm[:, 12:18].rearrange("p (d h) -> p d h", d=2)
        K.op(dve, [CP(lgp[0:64], lgv[0:64, :, :, 0]), CP(lgp[64:128], lgv[64:128, :, :, 1])], reads=[R_sm], writes=[R_sm])
        K.op(act, ACTF(sm[:, 18:24], sm[:, 12:18], AF.Exp, scale=128.0), reads=[R_sm], writes=[R_sm])
        K.op(act, [ACTF(sm[:, 24:30], sm[:, 0:6], AF.Exp, scale=ecol[:, 0:1], bias=LN8),
                   ACTF(sm[:, 30:36], sm[:, 6:12], AF.Exp, scale=ecol[:, 1:2], bias=LN8)],
             reads=[R_sm, R_tabs], writes=[R_sm])
        K.op(act, [ACTF(sm[:, 36:42], sm[:, 0:6], AF.Exp, scale=ecol[:, 2:3], bias=LN8),
                   ACTF(sm[:, 42:48], sm[:, 0:6], AF.Exp, scale=ecol[:, 0:1], bias=LN8),
                   ACTF(sm[:, 48:54], sm[:, 6:12], AF.Exp, scale=ecol[:, 1:2], bias=LN8),
                   ACTF(sm[:, 54:60], sm[:, 6:12], AF.Exp, scale=ecol[:, 3:4], bias=LN8)],
             reads=[R_sm, R_tabs], writes=[R_sm])
        fns = []
        for pr in range(3):
            fns.append(ACTF(XIF[:, pr, :], tabs[:, 256:384], AF.Exp, scale=sm[:, 12 + pr:13 + pr]))
            fns.append(ACTF(XIB[:, pr, :], tabs[:, 384:512], AF.Exp, scale=sm[:, 15 + pr:16 + pr]))
        K.op(act, fns, reads=[R_sm, R_t], writes=[R_tabs])
        dtt = T("dtt", [128, 6, 128], F32, 34 * KBY); R_dtt = K.res("dtt")
        K.op(dve, [TSM(dtt[:, h, :], tabs[:, 0:128], sm[:, h:h + 1]) for h in range(6)], reads=[R_sm, R_t], writes=[R_dtt])
        K.op(dve, [STT(dtt[:, h, :], tabs[:, 128:256], sm[:, 6 + h:7 + h], dtt[:, h, :], ALU.mult, ALU.add) for h in range(6)],
             reads=[R_sm, R_t, R_dtt], writes=[R_dtt])
        K.op(act, ACTF(DT[:], dtt[:], AF.Exp, bias=LN8), reads=[R_dtt], writes=[R_tabs])

        cin = T("cin", [128, 3, 8], F32, 40 * KBY); R_cin = K.res("cin")
        cs = T("cs", [128, 3, 8], BF16, 40 * KBY + 128); R_cs = K.res("cs")
        crep = T("crep", [128, 2, 8, 128], BF16, 41 * KBY); R_crep = K.res("crep")
        bmf = T("bmf", [128, 48], F32, 46 * KBY); R_bmf = K.res("bmf")
        bmbc = T("bmbc", [128, 2, 1024], F32, 47 * KBY); R_bmbc = K.res("bmbc")
        wmf = [T("wmf%d" % i, [128, 8, 512], F32, (56 + 16 * i) * KBY) for i in range(4)]
        R_wmf = [K.res("wmf%d" % i) for i in range(4)]
        wms = [T("wm%d" % i, [128, 8, 512], BF16, (176 + 8 * i) * KBY) for i in range(3)]
        R_wms = [K.res("wm%d" % i) for i in range(3)]
        crow = T("crow", [24, 128], F32, 37 * KBY); R_crow = K.res("crow")
        brow = T("brow", [48, 128], F32, 38 * KBY); R_brow = K.res("brow")
        K.dma(sp, crow[:], c_d.rearrange("r (kc p) -> (r kc) p", p=128), writes=[R_crow])
        K.dma(sp, brow[:], bmod_d[0].rearrange("(c p) -> c p", p=128), writes=[R_brow])
        PTx = K.psum("PTx", [128, 512], F32); R_PTx = K.res("PTx")
        K.op(pe, [TRP(PTx[:, 0:24], crow[:], idf[0:24, 0:24]), TRP(PTx[:, 64:112], brow[:], idf[0:48, 0:48])],
             reads=[R_crow, R_brow, R_idf], writes=[R_PTx])
        K.op(dve, [CP(cin[:].rearrange("p r k -> p (r k)"), PTx[:, 0:24]), CP(bmf[:], PTx[:, 64:112])], reads=[R_PTx], writes=[R_cin, R_bmf])
        K.dma(sp, bmbc[:, 0, :], bmod_d[:, 2048:3072].partition_broadcast(128), writes=[R_bmbc])
        K.dma(sp, bmbc[:, 1, :], bmod_d[:, 5120:6144].partition_broadcast(128), writes=[R_bmbc])
        K.op(act, ACTF(cs[:], cin[:], AF.Silu), reads=[R_cin], writes=[R_cs])
        K.op(pool, [CP(crep[:, b], cs[:, b, :].unsqueeze(2).to_broadcast([128, 8, 128])) for b in range(2)],
             reads=[R_cs], writes=[R_crep])
        PM = K.psum("PM", [128, 512], F32); R_PM = K.res("PM")
        PG = [K.psum("PG%d" % i, [128, 512], F32) for i in range(2)]
        R_PG = [K.res("PG%d" % i) for i in range(2)]
        wmv = wmod_d.rearrange("(kc p) c -> p kc c", p=128)
        order = [0, 1, 2, 3, 6, 7, 8, 9, 4, 5, 10, 11]
        kkmap = {0: 0, 1: 1, 3: 2, 4: 3}
        for n_, bi in enumerate(order):
            s = n_ % 3
            sf_ = n_ % 4
            K.dma(sp, wmf[sf_][:, 0:4, :], wmv[:, 0:4, bi * 512:(bi + 1) * 512], writes=[R_wmf[sf_]])
            K.dma(act, wmf[sf_][:, 4:8, :], wmv[:, 4:8, bi * 512:(bi + 1) * 512], writes=[R_wmf[sf_]])
            K.op(dve, CP(wms[s][:, 0:4, :], wmf[sf_][:, 0:4, :]), reads=[R_wmf[sf_]], writes=[R_wms[s]])
            K.op(act, ACTF(wms[s][:, 4:8, :], wmf[sf_][:, 4:8, :], AF.Copy), reads=[R_wmf[sf_]], writes=[R_wms[s]])
            m, half = bi // 2, bi % 2
            if m in kkmap:
                kk = kkmap[m]
                fns = []
                for fc in range(4):
                    col = (kk * 8 + half * 4 + fc) * 3
                    for kc in range(8):
                        fns.append(MM(PM[:, col:col + 3], wms[s][:, kc, fc * 128:(fc + 1) * 128], cs[:, :, kc],
                                      start=(kc == 0), stop=(kc == 7)))
                K.op(pe, fns, reads=[R_wms[s], R_cs], writes=[R_PM])
            else:
                gate = 0 if m == 2 else 1
                for b in range(2):
                    fns = [MM(PG[b][:], crep[:, b, kc, :], wms[s][:, kc, :], start=(kc == 0), stop=(kc == 7)) for kc in range(8)]
                    K.op(pe, fns, reads=[R_wms[s], R_crep], writes=[R_PG[b]])
                    K.op(dve, TT(gtbc[:, b, gate, half * 512:(half + 1) * 512], PG[b][:], bmbc[:, gate, half * 512:(half + 1) * 512], ALU.add),
                         reads=[R_PG[b], R_bmbc], writes=[R_gtbc])
        PMv = PM[:, 0:96].rearrange("p (k c r) -> p k r c", k=4, c=8, r=3)
        fns = []
        for m, kk in kkmap.items():
            fns.append(TT(modfm[:, :, kk, :], PMv[:, kk], bmf[:, m * 8:(m + 1) * 8].unsqueeze(1).to_broadcast([128, 3, 8]), ALU.add))
        K.op(dve, fns, reads=[R_PM, R_bmf], writes=[R_modfm])
        K.op(dve, [TSA(modfm[:, :, 1, :], modfm[:, :, 1, :], 1.0), TSA(modfm[:, :, 3, :], modfm[:, :, 3, :], 1.0)],
             reads=[R_modfm], writes=[R_modfm])
        if "modfm" in dbg_d:
            K.dma(sp, dbg_d["modfm"], modfm[:].rearrange("p a b c -> p (a b c)"), reads=[R_modfm])
        if "gtbc" in dbg_d:
            K.dma(sp, dbg_d["gtbc"], gtbc[:].rearrange("p a b c -> p (a b c)"), reads=[R_gtbc])
        if "sm" in dbg_d:
            K.dma(sp, dbg_d["sm"], sm[:], reads=[R_sm])
        K.barrier()
        K.release(m0)

    def ln_rstd(src, R_src, st, R_st):
        K.op(dve, [lambda e: e.bn_stats(out=st[:, 0:6], in_=src[:, 0:512]),
                   lambda e: e.bn_stats(out=st[:, 6:12], in_=src[:, 512:1024])], reads=[R_src], writes=[R_st])
        K.op(dve, lambda e: e.bn_aggr(out=st[:, 12:14], in_=st[:, 0:12]), reads=[R_st], writes=[R_st])
        K.op(dve, TSA(st[:, 14:15], st[:, 13:14], EPS), reads=[R_st], writes=[R_st])
        K.op(pool, TT(st[:, 14:15], st[:, 14:15], NHC, ALU.pow), reads=[R_st, R_sm], writes=[R_st])
        K.op(dve, TS(st[:, 15:16], st[:, 12:13], st[:, 14:15], -1.0, ALU.mult, ALU.mult), reads=[R_st], writes=[R_st])

    def phaseA(b, S):
        m0 = K.mark()
        rc = T("rc", [128, 16, 64], F32, 174 * KBY)
        rs = T("rs", [128, 16, 64], F32, 178 * KBY); R_rope = K.res("rope")
        base = 182 * KBY
        xt = [T("xt%d" % i, [128, 1024], F32, base + 4096 * i) for i in range(2)]; R_xt = [K.res("xt%d" % i) for i in range(2)]
        xn = [T("xn%d" % i, [128, 1024], BF16, base + 8192 + 2048 * i) for i in range(2)]; R_xn = [K.res("xn%d" % i) for i in range(2)]
        uT = [T("uT%d" % i, [128, 8, 128], BF16, base + 12288 + 2048 * i) for i in range(2)]; R_uT = [K.res("uT%d" % i) for i in range(2)]
        o2 = base + 16384
        t1 = T("t1", [128, 384], F32, o2); R_t1 = K.res("t1")
        t2 = T("t2", [128, 384], F32, o2 + 1536); R_t2 = K.res("t2")
        kr = T("kr", [128, 384], F32, o2 + 3072); R_kr = K.res("kr")
        qr = [T("qr%d" % i, [128, 384], BF16, o2 + 4608 + 768 * i) for i in range(2)]; R_qr = [K.res("qr%d" % i) for i in range(2)]
        krb = [T("krb%d" % i, [128, 384], BF16, o2 + 6144 + 768 * i) for i in range(2)]; R_krb = [K.res("krb%d" % i) for i in range(2)]
        kf = [T("kf%d" % i, [128, 384], BF16, o2 + 7680 + 768 * i) for i in range(2)]; R_kf = [K.res("kf%d" % i) for i in range(2)]
        stt = [T("stA%d" % i, [128, 16], F32, o2 + 9216 + 64 * i) for i in range(2)]; R_stt = [K.res("stA%d" % i) for i in range(2)]
        if b > 0:
            K.dma(sp, wi[:, 0:4, :], win_s[:, 0:4, :], reads=[R_wins], writes=[R_wi])
            K.dma(act, wi[:, 4:8, :], win_s[:, 4:8, :], reads=[R_wins], writes=[R_wi])
        pre = PRE if b == 0 else []
        K.dma(sp, rc[:].rearrange("p a b -> p (a b)"), rc_d, writes=[R_rope])
        K.dma(sp, rs[:].rearrange("p a b -> p (a b)"), rs_d, writes=[R_rope])
        PA = K.psum("PA", [128, 512], F32); PB = K.psum("PB", [128, 512], F32); PC = K.psum("PC", [128, 512], F32)
        PD = K.psum("PD", [128, 512], F32); PE_ = K.psum("PE", [128, 512], F32); PU = K.psum("PU", [128, 512], F32)
        PT1 = K.psum("PT1", [128, 1024], BF16); PT2 = K.psum("PT2", [128, 1024], BF16)
        R_PA, R_PB, R_PC, R_PD, R_PE, R_PU, R_PT1, R_PT2 = [K.res(n) for n in ("PA", "PB", "PC", "PD", "PE", "PU", "PT1", "PT2")]
        K.op(pool, [MSET(S["Sf"][:], 0.0), MSET(S["Sb"][:], 0.0)], writes=[S["R_Sf"], S["R_Sb"]])
        Vc, R_Vc = S["Vc"], S["R_Vc"]

        def proj(s, c0, c1, P, R_P):
            w = c1 - c0
            fns = [MM(P[:, 0:w], uT[s][:, kc, :], wi[:, kc, c0:c1], start=(kc == 0), stop=(kc == 7)) for kc in range(8)]
            K.op(pe, fns, reads=[R_uT[s], R_wi], writes=[R_P])

        def F1(ti):
            s = ti % 2
            isctx = ti < 2
            i = ti - 2
            src = ctx_d[b, ti * 128:(ti + 1) * 128, :] if isctx else x_d[b, i * 128:(i + 1) * 128, :]
            ox = K.dma(sp, xt[s][:], src, writes=[R_xt[s]])
            for _k in range(1):
                if pre:
                    dst_, src_, r_ = pre.pop(0)
                    K.dma(pool, dst_, src_, writes=[r_], after=[ox])
            ln_rstd(xt[s], R_xt[s], stt[s], R_stt[s])
            K.op(act, ACTF(xn[s][:], xt[s][:], AF.Identity, scale=stt[s][:, 14:15], bias=stt[s][:, 15:16]),
                 reads=[R_xt[s], R_stt[s]], writes=[R_xn[s]])

        def F2(ti):
            s = ti % 2
            row = 2 if ti < 2 else b
            K.op(pe, [TRP(PT1[:, kc * 128:(kc + 1) * 128], xn[s][:, kc * 128:(kc + 1) * 128], ident[:]) for kc in range(8)],
                 reads=[R_xn[s], R_ident], writes=[R_PT1])
            K.op(act, [ACTF(uT[s][:, kc, :], PT1[:, kc * 128:(kc + 1) * 128], AF.Identity,
                            scale=modfm[:, row, 1, kc:kc + 1], bias=modfm[:, row, 0, kc:kc + 1]) for kc in range(8)],
                 reads=[R_PT1, R_modfm], writes=[R_uT[s]])
            if ("uT" in dbg_d) and ti == 2 and b == 0:
                K.dma(sp, dbg_d["uT"], uT[s][:].rearrange("p a b -> p (a b)"), reads=[R_uT[s]])

        def Pst(ti):
            s = ti % 2
            isctx = ti < 2
            i = ti - 2
            if isctx:
                proj(s, 384, 768, PB, R_PB)
                proj(s, 768, 1280, PC, R_PC)
                proj(s, 1280, 1536, PD, R_PD)
                zf = sm[:, 36 + 6 * ti:42 + 6 * ti]
                zb = sm[:, 48 + 6 * ti:54 + 6 * ti]
                PBv = PB[:, 0:384].rearrange("p (h d) -> p h d", h=6)
                K.op(dve, [TT(kf[s][:].rearrange("p (h d) -> p h d", h=6), PBv, zf.unsqueeze(2).to_broadcast([128, 6, 64]), ALU.mult),
                           TT(qr[s][:].rearrange("p (h d) -> p h d", h=6), PBv, zb.unsqueeze(2).to_broadcast([128, 6, 64]), ALU.mult)],
                     reads=[R_PB, R_sm], writes=[R_kf[s], R_qr[s]])
                K.op(act, [ACTF(Vc[s][:, 0:512], PC[:], AF.Copy), ACTF(Vc[s][:, 512:768], PD[:, 0:256], AF.Copy)],
                     reads=[R_PC, R_PD], writes=[R_Vc[s]])
                return
            proj(s, 1792, 2304, PE_, R_PE)
            K.op(act, ACTF(S["SG"][:, i, 0:512], PE_[:], AF.Silu), reads=[R_PE], writes=[S["R_SG"][i]])
            proj(s, 0, 384, PA, R_PA)
            proj(s, 384, 768, PB, R_PB)
            proj(s, 2304, 2560, PE_, R_PE)
            K.op(act, ACTF(S["SG"][:, i, 512:768], PE_[:, 0:256], AF.Silu), reads=[R_PE], writes=[S["R_SG"][i]])
            proj(s, 768, 1280, PC, R_PC)
            proj(s, 1280, 1792, PD, R_PD)
            K.op(act, [ACTF(S["V"][:, i, 0:512], PC[:], AF.Copy), ACTF(S["V"][:, i, 512:768], PD[:, 0:256], AF.Copy),
                       ACTF(S["F"][:, i, :], PD[:, 256:512], AF.Copy)],
                 reads=[R_PC, R_PD], writes=[S["R_V"][i], S["R_F"][i]])
            cosb = rc[:, i, :].unsqueeze(1).to_broadcast([128, 6, 64])
            sinv = rs[:, i, :].rearrange("p (a b j) -> p a b j", a=2, b=2)

            def rope(P, R_P, outt, R_out):
                Pv = P[:, 0:384].rearrange("p (h d) -> p h d", h=6)
                P5 = P[:, 0:384].rearrange("p (h a b j) -> p h a b j", h=6, a=2, b=2)
                t25 = t2[:].rearrange("p (h a b j) -> p h a b j", h=6, a=2, b=2)
                K.op(dve, TT(t1[:].rearrange("p (h d) -> p h d", h=6), Pv, cosb, ALU.mult), reads=[R_P, R_rope], writes=[R_t1])
                K.op(dve, [TT(t25[:, :, :, 0, :], P5[:, :, :, 1, :], sinv[:, :, 0, :].unsqueeze(1).to_broadcast([128, 6, 2, 16]), ALU.mult),
                           TT(t25[:, :, :, 1, :], P5[:, :, :, 0, :], sinv[:, :, 1, :].unsqueeze(1).to_broadcast([128, 6, 2, 16]), ALU.mult)],
                     reads=[R_P, R_rope], writes=[R_t2])
                K.op(pool, TT(outt[:], t1[:], t2[:], ALU.add), reads=[R_t1, R_t2], writes=[R_out])

            rope(PA, R_PA, qr[s], R_qr[s])
            rope(PB, R_PB, kr, R_kr)
            K.op(pool, CP(krb[s][:], kr[:]), reads=[R_kr], writes=[R_krb[s]])
            krv = kr[:].rearrange("p (h d) -> p h d", h=6)
            K.op(pool, TT(S["KBs"][:, i, :].rearrange("p (h d) -> p h d", h=6), krv, sm[:, 30:36].unsqueeze(2).to_broadcast([128, 6, 64]), ALU.mult),
                 reads=[R_kr, R_sm], writes=[S["R_KBs"][i]])
            if i < 15:
                K.op(dve, TT(kf[s][:].rearrange("p (h d) -> p h d", h=6), krv, sm[:, 24:30].unsqueeze(2).to_broadcast([128, 6, 64]), ALU.mult),
                     reads=[R_kr, R_sm], writes=[R_kf[s]])

        def Bst(ti):
            s = ti % 2
            isctx = ti < 2
            i = ti - 2
            if isctx:
                fns = []
                for h in range(6):
                    fns.append(MM(PU[(h % 2) * 64:(h % 2 + 1) * 64, (h // 2) * 128:(h // 2 + 1) * 128],
                                  kf[s][:, h * 64:(h + 1) * 64], Vc[s][:, h * 128:(h + 1) * 128]))
                K.op(pe, fns, reads=[R_kf[s], R_Vc[s]], writes=[R_PU])
                K.op(dve, TT(S["Sf"][:], S["Sf"][:], PU[:, 0:384].rearrange("p (a e) -> p a e", a=3), ALU.add),
                     reads=[R_PU, S["R_Sf"]], writes=[S["R_Sf"]])
                fns = []
                for h in range(6):
                    fns.append(MM(PU[(h % 2) * 64:(h % 2 + 1) * 64, (h // 2) * 128:(h // 2 + 1) * 128],
                                  qr[s][:, h * 64:(h + 1) * 64], Vc[s][:, h * 128:(h + 1) * 128]))
                K.op(pe, fns, reads=[R_qr[s], R_Vc[s]], writes=[R_PU])
                K.op(dve, TT(S["Sb"][:], S["Sb"][:], PU[:, 0:384].rearrange("p (a e) -> p a e", a=3), ALU.add),
                     reads=[R_PU, S["R_Sb"]], writes=[S["R_Sb"]])
                return
            fns = [TRP(PT2[:, pr * 128:(pr + 1) * 128], qr[s][:, pr * 128:(pr + 1) * 128], ident[:]) for pr in range(3)]
            fns += [TRP(PT2[:, (3 + pr) * 128:(4 + pr) * 128], krb[s][:, pr * 128:(pr + 1) * 128], ident[:]) for pr in range(3)]
            K.op(pe, fns, reads=[R_qr[s], R_krb[s], R_ident], writes=[R_PT2])
            K.op(act, ACTF(S["QKT"][:, :, i * 128:(i + 1) * 128], PT2[:, 0:768].rearrange("p (a t) -> p a t", a=6), AF.Copy),
                 reads=[R_PT2], writes=[S["R_QKT"][i]])
            K.op(act, ACTF(S["SFb"][:, i], S["Sf"][:], AF.Copy), reads=[S["R_Sf"]], writes=[S["R_SFb"][i]])
            if i < 15:
                fns = []
                for h in range(6):
                    fns.append(MM(PU[(h % 2) * 64:(h % 2 + 1) * 64, (h // 2) * 128:(h // 2 + 1) * 128],
                                  kf[s][:, h * 64:(h + 1) * 64], S["V"][:, i, h * 128:(h + 1) * 128]))
                K.op(pe, fns, reads=[R_kf[s], S["R_V"][i]], writes=[R_PU])
                K.op(dve, [STT(S["Sf"][:, pr, :], S["Sf"][:, pr, :], sm[:, 18 + pr:19 + pr], PU[:, pr * 128:(pr + 1) * 128], ALU.mult, ALU.add)
                           for pr in range(3)], reads=[R_PU, S["R_Sf"], R_sm], writes=[S["R_Sf"]])

        F1(0)
        F1(1)
        F2(0)
        for ti in range(18):
            if ti + 2 < 18:
                F1(ti + 2)
            if ti + 1 < 18:
                F2(ti + 1)
            Pst(ti)
            if ti >= 1:
                Bst(ti - 1)
        Bst(17)
        K.barrier()
        K.release(m0)

    def phaseB(b, S):
        m0 = K.mark()
        base = 158 * KBY
        PTb = [T("PTb%d" % i, [128, 2, 3, 128], BF16, base + 1536 * i) for i in range(2)]; R_PTb = [K.res("PTb%d" % i) for i in range(2)]
        QF = [T("QF%d" % i, [128, 3, 128], BF16, base + 3072 + 768 * i) for i in range(2)]; R_QF = [K.res("QF%d" % i) for i in range(2)]
        QB = [T("QB%d" % i, [128, 3, 128], BF16, base + 4608 + 768 * i) for i in range(2)]; R_QB = [K.res("QB%d" % i) for i in range(2)]
        ysq = T("ysq", [128, 768], F32, base + 6144); R_ysq = K.res("ysq")
        sgr = T("sgr", [128, 768], F32, base + 9216); R_sgr = K.res("sgr")
        Rr = [T("Rr%d" % i, [128, 768], BF16, base + 12288 + 1536 * i) for i in range(2)]; R_Rr = [K.res("Rr%d" % i) for i in range(2)]
        R_Rrh = [[K.res("Rrh%d_%d" % (i, h)) for h in range(2)] for i in range(2)]
        rr = [T("rr%d" % i, [128, 8], F32, base + 15360 + 32 * i) for i in range(2)]; R_rr = [K.res("rr%d" % i) for i in range(2)]
        Sbb = [T("Sbb%d" % i, [128, 3, 128], BF16, base + 15424 + 768 * i) for i in range(2)]; R_Sbb = [K.res("Sbb%d" % i) for i in range(2)]
        ysb = [T("ysb%d" % i, [128, 768], F32, base + 16960 + 3072 * i) for i in range(2)]; R_ysb = [K.res("ysb%d" % i) for i in range(2)]
        _p0 = K.psum("PS0_0", [128, 512], F32); _p1 = K.psum("PS1_0", [128, 512], F32)
        PS0 = [_p0, _p0]; R_PS0 = [K.res("PS0_0"), K.res("PS0_0")]
        PS1 = [_p1, _p1]; R_PS1 = [K.res("PS1a"), K.res("PS1a")]
        NDUM = int(os.environ.get("NDUM", "8"))
        PDUM = K.psum("PDUM", [128, 512], F32); R_PDUM = K.res("PDUM")
        PY0 = K.psum("PY0", [128, 512], F32); PY1 = K.psum("PY1", [128, 512], F32); R_PY = K.res("PY")
        PU = K.psum("PUb", [128, 512], F32); R_PU = K.res("PUb")
        PTR = K.psum("PTR", [128, 1024], BF16); R_PTR = K.res("PTR")
        QKT, V, SG, SFb, KBs, RT = S["QKT"], S["V"], S["SG"], S["SFb"], S["KBs"], S["RT"]
        DTv = DT[:].rearrange("p (pr hh) c -> p hh pr c", hh=2)

        def S1(n):
            i = 15 - n; s = n % 2
            tok = slice(i * 128, (i + 1) * 128)
            fns = []
            for h in (0, 2, 4, 1, 3, 5):
                pr, hh = h // 2, h % 2
                ps = slice(hh * 64, (hh + 1) * 64)
                dst = (PS0[s] if hh == 0 else PS1[s])[:, pr * 128:(pr + 1) * 128]
                fns.append(MM(dst, QKT[ps, 3 + pr, tok], QKT[ps, pr, tok]))
            o1 = K.op(pe, fns, reads=[S["R_QKT"][i]], writes=[R_PS0[s], R_PS1[s]])
            if b == 0 and PRE:
                dst_, src_, r_ = PRE.pop(0)
                K.dma(pool, dst_, src_, writes=[r_], after=[o1])
            for _d in range(NDUM // 4):
                K.op(pe, [MM(PDUM[:], ident[:], V[:, 0, 0:512]) for _q in range(4)], reads=[R_ident], writes=[R_PDUM], after=[o1])
            K.op(dve, [TT(PTb[s][:, 0], PS0[s][:, 0:384].rearrange("p (a c) -> p a c", a=3), DTv[:, 0], ALU.mult),
                       TT(PTb[s][:, 1], PS1[s][:, 0:384].rearrange("p (a c) -> p a c", a=3), DTv[:, 1], ALU.mult)],
                 reads=[R_PS0[s], R_PS1[s], R_tabs], writes=[R_PTb[s]])
            K.op(pool, TT(QF[s][:], QKT[:, 0:3, tok], XIF[:], ALU.mult), reads=[S["R_QKT"][i], R_tabs], writes=[R_QF[s]])
            K.op(dve, TT(QB[s][:], QKT[:, 0:3, tok], XIB[:], ALU.mult), reads=[S["R_QKT"][i], R_tabs], writes=[R_QB[s]])

        def S2(n):
            i = 15 - n; s = n % 2
            K.op(act, ACTF(Sbb[s][:], S["Sb"][:], AF.Copy), reads=[S["R_Sb"]], writes=[R_Sbb[s]])
            fns = []
            for h in range(6):
                pr, hh = h // 2, h % 2
                ps = slice(hh * 64, (hh + 1) * 64)
                dst = PY0[:, h * 128:(h + 1) * 128] if h < 4 else PY1[:, (h - 4) * 128:(h - 3) * 128]
                fns.append(MM(dst, PTb[s][:, hh, pr, :], V[:, i, h * 128:(h + 1) * 128], start=True, stop=False))
                fns.append(MM(dst, QF[s][ps, pr, :], SFb[ps, i, pr, :], start=False, stop=False))
                fns.append(MM(dst, QB[s][ps, pr, :], Sbb[s][ps, pr, :], start=False, stop=True))
            K.op(pe, fns, reads=[R_PTb[s], S["R_V"][i], R_QF[s], R_QB[s], S["R_SFb"][i], R_Sbb[s]], writes=[R_PY])
            if i > 0:
                fns = []
                for h in range(6):
                    fns.append(MM(PU[(h % 2) * 64:(h % 2 + 1) * 64, (h // 2) * 128:(h // 2 + 1) * 128],
                                  KBs[:, i, h * 64:(h + 1) * 64], V[:, i, h * 128:(h + 1) * 128]))
                K.op(pe, fns, reads=[S["R_KBs"][i], S["R_V"][i]], writes=[R_PU])
                K.op(dve, [STT(S["Sb"][:, pr, :], S["Sb"][:, pr, :], sm[:, 21 + pr:22 + pr], PU[:, pr * 128:(pr + 1) * 128], ALU.mult, ALU.add)
                           for pr in range(3)], reads=[R_PU, S["R_Sb"], R_sm], writes=[S["R_Sb"]])
            K.op(act, [ACTF(ysb[s][:, 0:512], PY0[:], AF.Copy), ACTF(ysb[s][:, 512:768], PY1[:, 0:256], AF.Copy)],
                 reads=[R_PY], writes=[R_ysb[s], R_PY])

        def S3(n):
            i = 15 - n; s = n % 2
            K.op(act, ACTF(ysq[:], ysb[s][:], AF.Square), reads=[R_ysb[s]], writes=[R_ysq])
            K.op(dve, lambda e, s=s: e.tensor_reduce(out=rr[s][:, 0:6], in_=ysq[:].rearrange("p (h e) -> p h e", h=6), axis=AX.X, op=ALU.add),
                 reads=[R_ysq], writes=[R_rr[s]])
            K.op(act, ACTF(rr[s][:, 0:6], rr[s][:, 0:6], AF.Sqrt, scale=1.0 / 128.0, bias=EPSC), reads=[R_rr[s], R_sm], writes=[R_rr[s]])
            K.op(dve, _c(lambda e, s=s: e.reciprocal(out=rr[s][:, 0:6], in_=rr[s][:, 0:6]), 0.1), reads=[R_rr[s]], writes=[R_rr[s]])
            K.op(pool, TT(sgr[:].rearrange("p (h e) -> p h e", h=6), SG[:, i, :].rearrange("p (h e) -> p h e", h=6),
                          rr[s][:, 0:6].unsqueeze(2).to_broadcast([128, 6, 128]), ALU.mult),
                 reads=[S["R_SG"][i], R_rr[s]], writes=[R_sgr])
            K.op(dve, TT(Rr[s][:, 0:384], ysb[s][:, 0:384], sgr[:, 0:384], ALU.mult), reads=[R_ysb[s], R_sgr], writes=[R_Rrh[s][0]])
            K.op(pool, TT(Rr[s][:, 384:768], ysb[s][:, 384:768], sgr[:, 384:768], ALU.mult), reads=[R_ysb[s], R_sgr], writes=[R_Rrh[s][1]])
            if "gg" in dbg_d and b == 0:
                K.dma(sp, dbg_d["gg"][i * 128:(i + 1) * 128, :], Rr[s][:], reads=R_Rrh[s])

        def S4(n):
            i = 15 - n; s = n % 2
            tok = slice(i * 128, (i + 1) * 128)
            K.op(pe, [TRP(PTR[:, c * 128:(c + 1) * 128], Rr[s][:, c * 128:(c + 1) * 128], ident[:]) for c in range(6)],
                 reads=R_Rrh[s] + [R_ident], writes=[R_PTR])
            K.op(act, ACTF(RT[:, :, tok], PTR[:, 0:768].rearrange("p (a t) -> p a t", a=6), AF.Copy), reads=[R_PTR], writes=[S["R_RT"][i]])

        S1(0)
        for n in range(16 + 2):
            if n + 1 < 16:
                S1(n + 1)
            if n < 16:
                S2(n)
            if 0 <= n - 1 < 16:
                S3(n - 1)
            if 0 <= n - 2 < 16:
                S4(n - 2)
        K.barrier()
        K.release(m0)

    def phaseC(b, S):
        S["m0CD"] = K.mark()
        slots = [T("dfs%d" % i, [128, 8, 512], BF16, (24 + 8 * i) * KBY) for i in range(4)]
        R_sl = [K.res("dfs%d" % i) for i in range(4)]
        EO = T("EO", [128, 2, 8, 256], BF16, 60 * KBY); R_EO = [K.res("EO%d" % i) for i in range(8)]
        PC_ = [K.psum("PCd%d" % i, [128, 512], F32) for i in range(2)]; R_PC = [K.res("PCd%d" % i) for i in range(2)]
        F, AB = S["F"], S["AB"]
        S["R_sl"] = R_sl
        S["R_EO"] = R_EO
        for a in range(8):
            K.op(pool if a % 2 == 0 else dve, [TT(EO[:, 0, a, :], F[:, a, :], F[:, a + 8, :], ALU.add), TT(EO[:, 1, a, :], F[:, a, :], F[:, a + 8, :], ALU.subtract)],
                 reads=[S["R_F"][a], S["R_F"][a + 8]], writes=[R_EO[a]])
        n_ = 0
        for mb in range(2):
            for par in range(2):
                for mat in range(2):
                    s = n_ % 4
                    srcv = dft_d[par * 2 + mat].rearrange("(a p) n -> p a n", p=128)[:, :, mb * 512:(mb + 1) * 512]
                    K.dma(sp, slots[s][:, 0:4, :], srcv[:, 0:4, :], writes=[R_sl[s]])
                    K.dma(act, slots[s][:, 4:8, :], srcv[:, 4:8, :], writes=[R_sl[s]])
                    for cc in range(2):
                        ps = cc
                        fns = [MM(PC_[ps][:], EO[:, par, a, cc * 128:(cc + 1) * 128], slots[s][:, a, :], start=(a == 0), stop=(a == 7)) for a in range(8)]
                        K.op(pe, fns, reads=[R_sl[s]] + R_EO, writes=[R_PC[ps]])
                        dst = AB[:, mat * 2 + cc, :].rearrange("p (m two) -> p m two", two=2)[:, mb * 512:(mb + 1) * 512, par]
                        if cc == 0:
                            K.op(act, ACTF(dst, PC_[ps][:], AF.Copy), reads=[R_PC[ps]], writes=[S["R_ABm"][mb]])
                        else:
                            K.op(dve, CP(dst, PC_[ps][:]), reads=[R_PC[ps]], writes=[S["R_ABm"][mb]])
                    n_ += 1

    def phaseD(b, S):
        m0 = S["m0CD"]
        X1 = S["X1"]
        WO = T("WO", [128, 10, 1024], BF16, 88 * KBY); R_WOc = [K.res("WO%d" % i) for i in range(10)]
        lnb = T("lnb1", [128, 2, 1024], F32, 108 * KBY); R_lnb = K.res("lnb1")
        b1 = 116 * KBY
        wst = [T("wst%d" % i, [128, 1024], F32, b1 + 4096 * i) for i in range(2)]; R_wst = [K.res("wst%d" % i) for i in range(2)]
        cbd = T("cbdt", [128, 2, 2, 256], F32, 84 * KBY); R_cbd = K.res("cbdt")
        std = [T("stD%d" % i, [128, 16], F32, 23616 + 64 * i) for i in range(2)]; R_std = [K.res("stD%d" % i) for i in range(2)]
        b2 = 174 * KBY
        xr = [T("xr%d" % i, [128, 1024], F32, b2 + 4096 * i) for i in range(2)]; R_xr = [K.res("xr%d" % i) for i in range(2)]
        z1 = [T("z1_%d" % i, [128, 1024], F32, b2 + 8192 + 4096 * i) for i in range(2)]; R_z1 = [K.res("z1_%d" % i) for i in range(2)]
        R_z1h = [[K.res("z1h_%d_%d" % (i, h)) for h in range(2)] for i in range(2)]
        zn = [T("zn%d" % i, [128, 1024], F32, b2 + 16384 + 4096 * i) for i in range(2)]; R_zn = [K.res("zn%d" % i) for i in range(2)]
        wf = T("wf", [128, 2, 1024], F32, b2 + 24576); R_wf = K.res("wf")
        PO = [K.psum("PO%d" % i, [128, 512], F32) for i in range(4)]; R_PO = [K.res("PO%d" % i) for i in range(4)]
        PW = [K.psum("PW%d" % i, [128, 512], F32) for i in range(2)]; R_PW = [K.res("PW%d" % i) for i in range(2)]
        R_sl, R_EO = S["R_sl"], S["R_EO"]
        K.dma(sp, lnb[:, 0, :], ln_d[0:1, :].partition_broadcast(128), writes=[R_lnb])
        K.dma(sp, lnb[:, 1, :], ln_d[1:2, :].partition_broadcast(128), writes=[R_lnb])
        wov = wout_d.rearrange("(kc p) c -> p kc c", p=128)
        K.dma(sp, wf[:], wov[:, 0:2, :], writes=[R_wf])
        K.dma(sp, cbd[:].rearrange("p m a c -> p (m a) c"), cbd_d.rearrange("m (a p) c -> p (m a) c", p=128), writes=[R_cbd])
        n_ = 0
        for mat in range(2):
            for cp in range(2):
                for half in range(2):
                    ps = n_ % 2
                    fns = [MM(PW[ps][:], cbd[:, mat, a, cp * 128:(cp + 1) * 128], wf[:, a, half * 512:(half + 1) * 512],
                              start=(a == 0), stop=(a == 1)) for a in range(2)]
                    K.op(pe, fns, reads=[R_cbd, R_wf], writes=[R_PW[ps]])
                    K.op(dve, TT(WO[:, mat * 2 + cp, half * 512:(half + 1) * 512], PW[ps][:], gtbc[:, b, 0, half * 512:(half + 1) * 512], ALU.mult),
                         reads=[R_PW[ps], R_gtbc], writes=[R_WOc[mat * 2 + cp]])
                    n_ += 1
        for kc in range(6):
            s = kc % 2
            K.dma(sp, wst[s][:], wov[:, 2 + kc, :], writes=[R_wst[s]])
            K.op(pool if kc % 2 == 0 else dve, TT(WO[:, 4 + kc, :], wst[s][:], gtbc[:, b, 0, :], ALU.mult), reads=[R_wst[s], R_gtbc], writes=[R_WOc[4 + kc]])
        AB, RT = S["AB"], S["RT"]

        def XA(t):
            s = t % 2
            tok = slice(t * 128, (t + 1) * 128)
            oxr = K.dma(sp, xr[s][:], x_d[b, tok, :], writes=[R_xr[s]])
            if b == 0 and PRE:
                dst_, src_, r_ = PRE.pop(0)
                K.dma(pool, dst_, src_, writes=[r_], after=[oxr])
            for half in range(2):
                ps = s * 2 + half
                fns = []
                for c in range(10):
                    lhsT = AB[:, c, tok] if c < 4 else RT[:, c - 4, tok]
                    fns.append(MM(PO[ps][:], lhsT, WO[:, c, half * 512:(half + 1) * 512], start=(c == 0), stop=(c == 9)))
                K.op(pe, fns, reads=[S["R_ABm"][t // 8], S["R_RT"][t]] + R_WOc, writes=[R_PO[ps]])
                K.op(dve, STT(z1[s][:, half * 512:(half + 1) * 512], xr[s][:, half * 512:(half + 1) * 512], ALPHA, PO[ps][:], ALU.mult, ALU.add),
                     reads=[R_PO[ps], R_xr[s]], writes=[R_z1h[s][half]])
            st = std[s]
            K.op(dve, [_c(lambda e: e.bn_stats(out=st[:, 0:6], in_=z1[s][:, 0:512]), 0.7),
                       _c(lambda e: e.bn_stats(out=st[:, 6:12], in_=z1[s][:, 512:1024]), 0.7)], reads=R_z1h[s], writes=[R_std[s], R_z1[s]])
            K.op(dve, lambda e: e.bn_aggr(out=st[:, 12:14], in_=st[:, 0:12]), reads=[R_std[s]], writes=[R_std[s]])
            K.op(dve, TSA(st[:, 14:15], st[:, 13:14], EPS), reads=[R_std[s]], writes=[R_std[s]])
            K.op(pool, TT(st[:, 14:15], st[:, 14:15], NHC, ALU.pow), reads=[R_std[s], R_sm], writes=[R_std[s]])

        def XB(t):
            s = t % 2
            st = std[s]
            K.op(dve, TS(st[:, 15:16], st[:, 12:13], st[:, 14:15], -1.0, ALU.mult, ALU.mult), reads=[R_std[s]], writes=[R_std[s]])
            K.op(act, ACTF(zn[s][:], z1[s][:], AF.Identity, scale=st[:, 14:15], bias=st[:, 15:16]),
                 reads=[R_z1[s], R_std[s]] + R_z1h[s], writes=[R_zn[s]])
            K.op(pool, TT(zn[s][:], zn[s][:], lnb[:, 0, :], ALU.mult), reads=[R_zn[s], R_lnb], writes=[R_zn[s]])

        def Y(t):
            s = t % 2
            tok = slice(t * 128, (t + 1) * 128)
            alias = [R_sl[t // 2]] if t < 8 else (R_EO if t in (9, 10) else ([R_cbd] if t == 15 else []))
            K.op(dve, TT(X1[:, t, :], zn[s][:], lnb[:, 1, :], ALU.add), reads=[R_zn[s], R_lnb], writes=[S["R_X1"][t]] + alias)
            if "x1" in dbg_d and b == 0:
                K.dma(sp, dbg_d["x1"][tok, :], X1[:, t, :], reads=[S["R_X1"][t]])

        XA(0)
        XA(1)
        XB(0)
        for t in range(NT):
            if t + 2 < NT:
                XA(t + 2)
            if t + 1 < NT:
                XB(t + 1)
            Y(t)
        K.barrier()
        K.release(m0)

    def phaseE(b, S):
        m0 = K.mark()
        X1 = S["X1"]
        WD = T("WD", [128, NJ, 1024], BF16, 88 * KBY); R_WD = K.res("WD")
        HT = T("HT", [128, NJ, 512], BF16, 132 * KBY); R_HT = [K.res("HT%d" % j) for j in range(NJ)]
        u2T = T("u2T", [128, 8, 512], BF16, 154 * KBY); R_u2T = [K.res("u2T%d" % i) for i in range(4)]
        ring = [T("ring%d" % i, [128, 2, 8, 128], BF16, 162 * KBY + 4096 * i) for i in range(3)]; R_ring = [K.res("ring%d" % i) for i in range(3)]
        lnb = T("lnb2", [128, 2, 1024], F32, 174 * KBY); R_lnb = K.res("lnb2")
        b2 = 182 * KBY
        xn = T("xnE", [128, 1024], BF16, b2); R_xn = K.res("xnE")
        sgt = [T("sgt%d" % i, [128, 512], F32, b2 + 2048 + 2048 * i) for i in range(2)]; R_sgt = [K.res("sgt%d" % i) for i in range(2)]
        z1 = T("z1E", [128, 1024], F32, b2 + 6144); R_z1 = K.res("z1E")
        zn = T("znE", [128, 1024], F32, b2 + 10240); R_zn = K.res("znE")
        ost = [T("ost%d" % i, [128, 1024], F32, b2 + 14336 + 4096 * i) for i in range(2)]; R_ost = [K.res("ost%d" % i) for i in range(2)]
        ste = [T("stE%d" % i, [128, 16], F32, b2 + 22528 + 64 * i) for i in range(4)]; R_ste = [K.res("stE%d" % i) for i in range(4)]
        PT = K.psum("PTE", [128, 1024], BF16); R_PT = K.res("PTE")
        PG = [K.psum("PGE%d" % i, [128, 512], F32) for i in range(2)]; R_PG = [K.res("PGE%d" % i) for i in range(2)]
        PUu = [K.psum("PUE%d" % i, [128, 512], F32) for i in range(2)]; R_PUu = [K.res("PUE%d" % i) for i in range(2)]
        PO = [K.psum("POE%d" % i, [128, 512], F32) for i in range(2)]; R_PO = [K.res("POE%d" % i) for i in range(2)]
        K.dma(sp, lnb[:, 0, :], ln_d[2:3, :].partition_broadcast(128), writes=[R_lnb])
        K.dma(sp, lnb[:, 1, :], ln_d[3:4, :].partition_broadcast(128), writes=[R_lnb])
        wdv = wd_d.rearrange("(j p) c -> p j c", p=128)
        K.dma(pool, WD[:, 0:11, :], wdv[:, 0:11, :], writes=[R_WD])
        K.dma(pool, WD[:, 11:22, :], wdv[:, 11:22, :], writes=[R_WD])
        items = [(bk, j) for bk in range(4) for j in range(NJ)]
        loaded = [0]

        def ensure(n):
            while loaded[0] <= min(n, len(items) - 1):
                r = loaded[0] % 3
                K.dma(sp, ring[r][:], wgu_s[items[loaded[0]][1]], reads=[R_wgu], writes=[R_ring[r]])
                loaded[0] += 1

        def ln_in(t, tl):
            st = ste[t % 2]; R_st = R_ste[t % 2]
            ln_rstd(X1[:, t, :], S["R_X1"][t], st, R_st)
            K.op(act, ACTF(xn[:], X1[:, t, :], AF.Identity, scale=st[:, 14:15], bias=st[:, 15:16]),
                 reads=[S["R_X1"][t], R_st], writes=[R_xn])
            K.op(pe, [TRP(PT[:, kc * 128:(kc + 1) * 128], xn[:, kc * 128:(kc + 1) * 128], ident[:]) for kc in range(8)],
                 reads=[R_xn, R_ident], writes=[R_PT])
            K.op(act, [ACTF(u2T[:, kc, tl * 128:(tl + 1) * 128], PT[:, kc * 128:(kc + 1) * 128], AF.Identity,
                            scale=modfm[:, b, 3, kc:kc + 1], bias=modfm[:, b, 2, kc:kc + 1]) for kc in range(8)],
                 reads=[R_PT, R_modfm], writes=[R_u2T[tl]])

        ensure(1)
        for tl in range(4):
            ln_in(tl, tl)
        for bk in range(4):
            for j in range(NJ):
                n = bk * NJ + j
                ensure(n + 2)
                r = n % 3
                p = j % 2
                K.op(pe, [MM(PG[p][:], ring[r][:, 0, kc, :], u2T[:, kc, :], start=(kc == 0), stop=(kc == 7)) for kc in range(8)],
                     reads=[R_ring[r]] + R_u2T, writes=[R_PG[p]])
                K.op(pe, [MM(PUu[p][:], ring[r][:, 1, kc, :], u2T[:, kc, :], start=(kc == 0), stop=(kc == 7)) for kc in range(8)],
                     reads=[R_ring[r]] + R_u2T, writes=[R_PUu[p]])
                K.op(act, ACTF(sgt[p][:], PG[p][:], AF.Silu), reads=[R_PG[p]], writes=[R_sgt[p]])
                K.op(dve, TT(HT[:, j, :], PUu[p][:], sgt[p][:], ALU.mult), reads=[R_PUu[p], R_sgt[p]], writes=[R_HT[j]])
            for tl in range(4):
                t = bk * 4 + tl
                tok = slice(t * 128, (t + 1) * 128)
                POs, R_POs = (PO, R_PO) if tl % 2 == 0 else (PUu, R_PUu)
                for half in range(2):
                    fns = [MM(POs[half][:], HT[:, j, tl * 128:(tl + 1) * 128], WD[:, j, half * 512:(half + 1) * 512],
                              start=(j == 0), stop=(j == NJ - 1)) for j in range(NJ)]
                    K.op(pe, fns, reads=R_HT + [R_WD], writes=[R_POs[half]])
                if bk < 3:
                    ln_in((bk + 1) * 4 + tl, tl)
                for half in range(2):
                    K.op(dve, TT(z1[:, half * 512:(half + 1) * 512], POs[half][:], gtbc[:, b, 1, half * 512:(half + 1) * 512], ALU.mult),
                         reads=[R_POs[half], R_gtbc], writes=[R_z1])
                K.op(dve, STT(z1[:], X1[:, t, :], ALPHA, z1[:], ALU.mult, ALU.add), reads=[S["R_X1"][t], R_z1], writes=[R_z1])
                st = ste[2 + t % 2]; R_st = R_ste[2 + t % 2]
                ln_rstd(z1, R_z1, st, R_st)
                K.op(act, ACTF(zn[:], z1[:], AF.Identity, scale=st[:, 14:15], bias=st[:, 15:16]), reads=[R_z1, R_st], writes=[R_zn])
                K.op(pool, TT(zn[:], zn[:], lnb[:, 0, :], ALU.mult), reads=[R_zn, R_lnb], writes=[R_zn])
                o = t % 2
                K.op(dve, TT(ost[o][:], zn[:], lnb[:, 1, :], ALU.add), reads=[R_zn, R_lnb], writes=[R_ost[o]])
                K.dma(sp, out_d[b, tok, :], ost[o][:], reads=[R_ost[o]])
        K.barrier()
        K.release(m0)

    phase0()
    for b in range(nb):
        S = {}
        S["Sf"] = T("Sf%d" % b, [128, 3, 128], F32, 24 * KBY); S["R_Sf"] = K.res("Sf")
        S["Sb"] = T("Sb%d" % b, [128, 3, 128], F32, 24 * KBY + 1536); S["R_Sb"] = K.res("Sb")
        S["Vc"] = [T("Vc%d_%d" % (b, i), [128, 768], BF16, 24 * KBY + 3072 + 1536 * i) for i in range(2)]; S["R_Vc"] = [K.res("Vc%d" % i) for i in range(2)]
        S["QKT"] = T("QKT%d" % b, [128, 6, N], BF16, 30 * KBY); S["R_QKT"] = [K.res("QKT%d" % i) for i in range(NT)]
        S["KBs"] = T("KBs%d" % b, [128, NT, 384], BF16, 54 * KBY); S["R_KBs"] = [K.res("KBs%d" % i) for i in range(NT)]
        S["V"] = T("V%d" % b, [128, NT, 768], BF16, 66 * KBY); S["R_V"] = [K.res("V%d" % i) for i in range(NT)]
        S["SG"] = T("SG%d" % b, [128, NT, 768], BF16, 90 * KBY); S["R_SG"] = [K.res("SG%d" % i) for i in range(NT)]
        S["SFb"] = T("SFb%d" % b, [128, NT, 3, 128], BF16, 114 * KBY); S["R_SFb"] = [K.res("SFb%d" % i) for i in range(NT)]
        S["F"] = T("F%d" % b, [128, NT, 256], BF16, 126 * KBY); S["R_F"] = [K.res("F%d" % i) for i in range(NT)]
        S["RT"] = T("RT%d" % b, [128, 6, N], BF16, 134 * KBY); S["R_RT"] = [K.res("RT%d" % i) for i in range(NT)]
        S["AB"] = T("AB%d" % b, [128, 4, N], BF16, 158 * KBY); S["R_AB"] = K.res("AB"); S["R_ABm"] = [K.res("ABm0"), K.res("ABm1")]
        S["X1"] = T("X1_%d" % b, [128, NT, 1024], F32, 24 * KBY); S["R_X1"] = [K.res("X1_%d" % i) for i in range(NT)]
        if "A" in phases:
            phaseA(b, S)
        if "B" in phases:
            phaseB(b, S)
        if "C" in phases:
            phaseC(b, S)
        if "D" in phases:
            phaseD(b, S)
        if "E" in phases:
            phaseE(b, S)
    assert not PRE, len(PRE)
    K.emit()
    K.close()
    print("instr counts:", {e.name: (e.ninstr, e.n) for e in K.engs}, "sems:", len(K.sems), "sim_us:", round(getattr(K, "sim_time", 0.0), 1))
    return nc


def _consts():
    p = np.arange(128)
    t = (np.arange(16)[None, :] * 128 + p[:, None]).astype(np.float64)
    rowp = np.floor(t / 64.0)
    colp = np.mod(t, 64.0)
    fr = 10000.0 ** (-np.arange(16, dtype=np.float64) / 16.0)
    ar = rowp[:, :, None] * fr[None, None, :]
    ac = colp[:, :, None] * fr[None, None, :]
    cos64 = np.concatenate([np.cos(ar), np.cos(ar), np.cos(ac), np.cos(ac)], axis=-1)
    sin64 = np.concatenate([-np.sin(ar), np.sin(ar), -np.sin(ac), np.sin(ac)], axis=-1)
    rope_cos = cos64.reshape(128, 1024).astype(np.float32)
    rope_sin = sin64.reshape(128, 1024).astype(np.float32)
    nn = np.arange(N // 2, dtype=np.float64)
    sc = 1.0 / np.sqrt(N * 64.0)
    mats = []
    for par in range(2):
        ang = 2.0 * np.pi * np.mod(np.outer(nn, 2.0 * nn + par), N) / N
        mats.append(np.cos(ang) * sc)
        mats.append(np.sin(ang) * sc)
    dft = np.stack(mats).astype(ml_dtypes.bfloat16)
    c64 = np.arange(64, dtype=np.float64)
    a64 = 2.0 * np.pi * np.mod(np.outer(c64, c64), 64) / 64.0
    C64, S64 = np.cos(a64), np.sin(a64)
    cbd = np.zeros((2, 256, 256), np.float64)
    for g in range(4):
        cbd[0, g * 64:(g + 1) * 64, g * 64:(g + 1) * 64] = C64
        cbd[1, g * 64:(g + 1) * 64, g * 64:(g + 1) * 64] = -S64
    cbd = cbd.astype(np.float32)
    l = np.arange(128)[:, None].astype(np.float64)
    c = np.arange(128)[None, :].astype(np.float64)
    diffp = np.maximum(c - l, 0.0)
    diffn = np.maximum(l - c, 0.0)
    idxa = np.broadcast_to(c + 1.0, (128, 128))
    idxb = np.broadcast_to(128.0 - c, (128, 128))
    pp = np.arange(128, dtype=np.float64)[:, None]
    ecol = np.concatenate([127.0 - pp, pp, 255.0 - pp, 128.0 + pp], axis=1)
    tabs = np.concatenate([diffp, diffn, idxa, idxb, ecol], axis=1).astype(np.float32)
    ident = np.eye(128, dtype=np.float32)
    return dict(rope_cos=rope_cos, rope_sin=rope_sin, dft=dft, cbd=cbd, tabs=tabs, ident=ident)


_CACHE = {}


def kernel(x, c, ctx, c_ctx, w_mod, b_mod, w_in, w_out, decay_fwd, decay_bwd,
           ln1_g, ln1_b, w_ffn_gate, w_ffn_up, w_ffn_down, ln2_g, ln2_b):
    f = lambda a: np.ascontiguousarray(np.asarray(a, dtype=np.float32))
    x, c, ctx, c_ctx = f(x), f(c), f(ctx), f(c_ctx)
    cons = _consts()
    shared = dict(
        w_mod=f(w_mod)[0], b_mod=f(b_mod)[0:1], w_in=f(w_in)[0], w_out=f(w_out)[0],
        decay=np.concatenate([f(decay_fwd)[0], f(decay_bwd)[0]])[None, :].copy(),
        lnp=np.stack([f(ln1_g)[0], f(ln1_b)[0], f(ln2_g)[0], f(ln2_b)[0]]),
        w_g=f(w_ffn_gate)[0], w_u=f(w_ffn_up)[0], w_d=f(w_ffn_down)[0], **cons)
    nc = build_program()
    in_maps = []
    for i in range(8):
        m = dict(shared)
        m["x"] = x[2 * i:2 * i + 2]
        m["ctx"] = ctx[2 * i:2 * i + 2]
        m["c3"] = np.concatenate([c[2 * i:2 * i + 2], c_ctx[None, :]], axis=0)
        in_maps.append(m)
    res = run_bass_kernel_spmd(nc, in_maps, core_ids=list(range(8)))
    out = np.concatenate([r["out"] for r in res.results], axis=0)
    return out.astype(np.float32)
```
